# Optimizing a Trainium2 kernel written in Bass

```python
import math
import jax, jax.numpy as jnp
from jax import lax
import numpy as np

D_MODEL = 2048
BATCH = 4
SEQ = 2048
DEPTH = 2

HEAD_DIM = 64
GROUP_WIDTH = D_MODEL // 4
CONV_WIDTH = GROUP_WIDTH
CONV_K = 3
SWA_HEADS = GROUP_WIDTH // HEAD_DIM
SWA_KV_HEADS = max(1, SWA_HEADS // 4)
SWA_GROUP = SWA_HEADS // SWA_KV_HEADS
SWA_WIDTH = SWA_HEADS * HEAD_DIM
SWA_KV_WIDTH = SWA_KV_HEADS * HEAD_DIM
SWA_WINDOW = 128
DIL_HEADS = GROUP_WIDTH // HEAD_DIM
DIL_WIDTH = DIL_HEADS * HEAD_DIM
DIL_PAIRS = ((128, 1), (512, 4), (2048, 16))
RWKV_HEADS = GROUP_WIDTH // HEAD_DIM
RWKV_WIDTH = RWKV_HEADS * HEAD_DIM
DECAY_LORA = 64
ICLR_LORA = 64
VRES_LORA = 32
GATE_LORA = 128
RWKV_IN_WIDTH = 3 * RWKV_WIDTH + DECAY_LORA + ICLR_LORA + GATE_LORA
MIX_WIDTH = CONV_WIDTH + SWA_WIDTH + DIL_WIDTH + RWKV_WIDTH
IN_SPLITS = (CONV_WIDTH, CONV_WIDTH, CONV_WIDTH,
             SWA_WIDTH, SWA_KV_WIDTH, SWA_KV_WIDTH,
             DIL_WIDTH, DIL_WIDTH, DIL_WIDTH,
             RWKV_IN_WIDTH)
IN_WIDTH = sum(IN_SPLITS)
D_FF = 4 * D_MODEL
BLK = 128
NUM_BUCKETS = 32
BUCKET_MAX_DIST = 128
N_ATTN_HEADS = SWA_HEADS + DIL_HEADS
RMS_EPS = 1e-6
LN_X_EPS = 64e-5
NEG = -1e30

kernel_name = 'hybrid_parallel_heads_block'


def split_cols(t, sizes):
    out, start = [], 0
    for s in sizes:
        out.append(t[..., start:start + s])
        start += s
    return out


def rms_norm(x, g, eps=RMS_EPS):
    xf = x.astype(jnp.float32)
    y = xf * lax.rsqrt(jnp.mean(xf * xf, axis=-1, keepdims=True) + eps)
    return y.astype(x.dtype) * g


def t5_bucket(dist):
    dist = jnp.maximum(dist, 0)
    max_exact = NUM_BUCKETS // 2
    scaled = jnp.log(jnp.maximum(dist, 1).astype(jnp.float32) / max_exact) / math.log(BUCKET_MAX_DIST / max_exact)
    large = max_exact + (scaled * (NUM_BUCKETS - max_exact)).astype(jnp.int32)
    large = jnp.minimum(large, NUM_BUCKETS - 1)
    return jnp.where(dist < max_exact, dist, large)


def block_rel_bias(table_cols, stride):
    dist = BLK + jnp.arange(BLK)[:, None] - jnp.arange(2 * BLK)[None, :]
    bucket = t5_bucket(dist * stride)
    return jnp.transpose(table_cols[bucket].astype(jnp.float32), (2, 0, 1))


def banded_attention(q, k, v, bias, max_dist, sink=None):
    n, hk, g, seq, dh = q.shape
    nb = -(-seq // BLK)
    pad = nb * BLK - seq
    qb = jnp.pad(q, ((0, 0), (0, 0), (0, 0), (0, pad), (0, 0))).reshape(n, hk, g, nb, BLK, dh)

    def windows(t):
        tb = jnp.pad(t, ((0, 0), (0, 0), (BLK, pad), (0, 0))).reshape(n, hk, nb + 1, BLK, dh)
        return jnp.concatenate([tb[:, :, :-1], tb[:, :, 1:]], axis=3)

    kw, vw = windows(k), windows(v)
    s = jnp.einsum('nhgiqd,nhikd->nhgiqk', qb, kw, preferred_element_type=jnp.float32) * (dh ** -0.5)
    s = s + bias[None, :, :, None]
    dist = BLK + jnp.arange(BLK)[:, None] - jnp.arange(2 * BLK)[None, :]
    kpos = (jnp.arange(nb)[:, None] - 1) * BLK + jnp.arange(2 * BLK)[None, :]
    valid = ((dist >= 0) & (dist <= max_dist))[None] & (kpos >= 0)[:, None, :]
    s = jnp.where(valid, s, NEG)
    m = jnp.max(s, axis=-1, keepdims=True)
    if sink is not None:
        sk = sink.astype(jnp.float32)[None, :, :, None, None, None]
        m = jnp.maximum(m, sk)
    p = jnp.exp(s - m)
    den = jnp.sum(p, axis=-1, keepdims=True)
    if sink is not None:
        den = den + jnp.exp(sk - m)
    o = jnp.einsum('nhgiqk,nhikd->nhgiqd', p, vw.astype(jnp.float32)) / den
    lse = (m + jnp.log(den))[..., 0]
    o = o.reshape(n, hk, g, nb * BLK, dh)[:, :, :, :seq].astype(q.dtype)
    lse = lse.reshape(n, hk, g, nb * BLK)[..., :seq]
    return o, lse


def dilate(t, r):
    b, h, s, d = t.shape
    return t.reshape(b, h, s // r, r, d).transpose(0, 3, 1, 2, 4).reshape(b * r, h, s // r, d)


def undilate(t, r):
    br, h, l, d = t.shape
    return t.reshape(br // r, r, h, l, d).transpose(0, 2, 3, 1, 4).reshape(br // r, h, l * r, d)


def short_conv_mixer(gate_b, gate_c, u, conv_w):
    z = gate_c * u
    z = lax.conv_general_dilated(z, conv_w[:, None, :], (1,), [(CONV_K - 1, 0)],
                                 dimension_numbers=('NWC', 'WIO', 'NWC'),
                                 feature_group_count=CONV_WIDTH)
    return gate_b * z


def swa_mixer(q, k, v, q_gain, k_gain, sink, bias):
    bsz, seq, _ = q.shape
    q = rms_norm(q.reshape(bsz, seq, SWA_HEADS, HEAD_DIM), q_gain)
    k = rms_norm(k.reshape(bsz, seq, SWA_KV_HEADS, HEAD_DIM), k_gain)
    v = v.reshape(bsz, seq, SWA_KV_HEADS, HEAD_DIM)
    q = q.transpose(0, 2, 1, 3).reshape(bsz, SWA_KV_HEADS, SWA_GROUP, seq, HEAD_DIM)
    o, _ = banded_attention(q, k.transpose(0, 2, 1, 3), v.transpose(0, 2, 1, 3), bias,
                            SWA_WINDOW - 1, sink.reshape(SWA_KV_HEADS, SWA_GROUP))
    return o.reshape(bsz, SWA_HEADS, seq, HEAD_DIM).transpose(0, 2, 1, 3).reshape(bsz, seq, SWA_WIDTH)


def dilated_mixer(q, k, v, q_gain, k_gain, biases):
    bsz, seq, _ = q.shape
    q = rms_norm(q.reshape(bsz, seq, DIL_HEADS, HEAD_DIM), q_gain).transpose(0, 2, 1, 3)
    k = rms_norm(k.reshape(bsz, seq, DIL_HEADS, HEAD_DIM), k_gain).transpose(0, 2, 1, 3)
    v = v.reshape(bsz, seq, DIL_HEADS, HEAD_DIM).transpose(0, 2, 1, 3)
    outs, lses = [], []
    for (window, r), bias in zip(DIL_PAIRS, biases):
        o, lse = banded_attention(dilate(q, r)[:, :, None], dilate(k, r), dilate(v, r), bias, window // r)
        outs.append(undilate(o[:, :, 0], r))
        lses.append(undilate(lse[:, :, 0, :, None], r))
    wts = jax.nn.softmax(jnp.stack(lses), axis=0)
    o = jnp.sum(wts * jnp.stack(outs).astype(jnp.float32), axis=0).astype(q.dtype)
    return o.transpose(0, 2, 1, 3).reshape(bsz, seq, DIL_WIDTH)


def wkv7_scan(r, w, k, v, a, b):
    bsz, seq, nh, n = r.shape

    def step(state, inp):
        r_t, w_t, k_t, v_t, a_t, b_t = inp
        sa = jnp.einsum('bhvk,bhk->bhv', state, a_t)
        state = (state * w_t[:, :, None, :] + sa[..., None] * b_t[:, :, None, :]
                 + v_t[..., None] * k_t[:, :, None, :])
        return state, jnp.einsum('bhvk,bhk->bhv', state, r_t)

    xs = tuple(jnp.swapaxes(t, 0, 1) for t in (r, w, k, v, a, b))
    state0 = jnp.zeros((bsz, nh, n, n), jnp.float32)
    _, y = lax.scan(step, state0, xs)
    return jnp.swapaxes(y, 0, 1)


def rwkv7_mixer(p, mu, w0, w2, a0, a2, g2, k_k, k_a, r_k, ln_g, ln_b, v_first, vres):
    bsz, seq, _ = p.shape
    prev = jnp.pad(p, ((0, 0), (1, 0), (0, 0)))[:, :seq]
    p = p + (prev - p) * mu
    r, k, v, wd, ad, gd = split_cols(p, (RWKV_WIDTH, RWKV_WIDTH, RWKV_WIDTH, DECAY_LORA, ICLR_LORA, GATE_LORA))
    logw = -jax.nn.softplus(-(w0 + jnp.tanh(wd) @ w2)) - 0.5
    decay = jnp.exp(-jnp.exp(logw.astype(jnp.float32)))
    a = jax.nn.sigmoid(a0 + ad @ a2)
    g = jax.nn.sigmoid(gd) @ g2
    if vres is None:
        v_first = v
    else:
        v0, v1, v2 = vres
        v = v + (v_first - v) * jax.nn.sigmoid(v0 + (v @ v1) @ v2)

    def heads(t):
        return t.reshape(bsz, seq, RWKV_HEADS, HEAD_DIM).astype(jnp.float32)

    kk = heads(k * k_k)
    kk = kk * lax.rsqrt(jnp.maximum(jnp.sum(kk * kk, axis=-1, keepdims=True), 1e-24))
    k = k * (1 + (a - 1) * k_a)
    rh, kh, vh, ah, wh = heads(r), heads(k), heads(v), heads(a), heads(decay)
    y = wkv7_scan(rh, wh, kh, vh, -kk, kk * ah)
    mean = jnp.mean(y, axis=-1, keepdims=True)
    var = jnp.mean(jnp.square(y - mean), axis=-1, keepdims=True)
    y = (y - mean) * lax.rsqrt(var + LN_X_EPS)
    y = y.reshape(bsz, seq, RWKV_WIDTH) * ln_g + ln_b
    bonus = (jnp.sum(rh * kh * r_k.astype(jnp.float32), axis=-1, keepdims=True) * vh).reshape(bsz, seq, RWKV_WIDTH)
    y = (y + bonus).astype(p.dtype) * g
    return y, v_first


def setup_inputs(seed: int = 0) -> dict:
    key = jax.random.key(seed)
    ks = iter(jax.random.split(key, 40))

    def nrm(shape, scale):
        return jax.random.normal(next(ks), shape, jnp.float32) * scale

    def unif(shape, lo, hi):
        return jax.random.uniform(next(ks), shape, jnp.float32, minval=lo, maxval=hi)

    L, W = DEPTH, RWKV_WIDTH
    return {
        'x': nrm((BATCH, SEQ, D_MODEL), 1.0),
        'norm_mix': 1.0 + nrm((L, D_MODEL), 0.02),
        'w_in': nrm((L, D_MODEL, IN_WIDTH), D_MODEL ** -0.5),
        'conv_w': nrm((L, CONV_K, CONV_WIDTH), CONV_K ** -0.5),
        'swa_q_norm': 1.0 + nrm((L, HEAD_DIM), 0.02),
        'swa_k_norm': 1.0 + nrm((L, HEAD_DIM), 0.02),
        'swa_sink': nrm((L, SWA_HEADS), 0.5),
        'dil_q_norm': 1.0 + nrm((L, HEAD_DIM), 0.02),
        'dil_k_norm': 1.0 + nrm((L, HEAD_DIM), 0.02),
        'rwkv_mu': unif((L, RWKV_IN_WIDTH), 0.0, 1.0),
        'decay_w0': unif((L, W), -3.0, 1.0),
        'decay_w2': nrm((L, DECAY_LORA, W), 0.1),
        'iclr_a0': nrm((L, W), 0.1),
        'iclr_a2': nrm((L, ICLR_LORA, W), 0.1),
        'gate_g2': nrm((L, GATE_LORA, W), GATE_LORA ** -0.5),
        'k_k': 0.85 + nrm((L, W), 0.02),
        'k_a': 1.0 + nrm((L, W), 0.02),
        'r_k': nrm((L, RWKV_HEADS, HEAD_DIM), 0.1),
        'ln_x_g': 1.0 + nrm((L, W), 0.02),
        'ln_x_b': nrm((L, W), 0.02),
        'vres_v0': nrm((L - 1, W), 0.1),
        'vres_v1': nrm((L - 1, W, VRES_LORA), W ** -0.5),
        'vres_v2': nrm((L - 1, VRES_LORA, W), 0.1),
        'w_out': nrm((L, MIX_WIDTH, D_MODEL), MIX_WIDTH ** -0.5),
        'norm_ffn': 1.0 + nrm((L, D_MODEL), 0.02),
        'w_up': nrm((L, D_MODEL, D_FF), D_MODEL ** -0.5),
        'w_down': nrm((L, D_FF, D_MODEL), D_FF ** -0.5),
        'rel_bias': nrm((NUM_BUCKETS, N_ATTN_HEADS), 0.3),
    }


def reference(x, norm_mix, w_in, conv_w, swa_q_norm, swa_k_norm, swa_sink, dil_q_norm, dil_k_norm,
              rwkv_mu, decay_w0, decay_w2, iclr_a0, iclr_a2, gate_g2, k_k, k_a, r_k, ln_x_g, ln_x_b,
              vres_v0, vres_v1, vres_v2, w_out, norm_ffn, w_up, w_down, rel_bias):
    swa_bias = block_rel_bias(rel_bias[:, :SWA_HEADS], 1).reshape(SWA_KV_HEADS, SWA_GROUP, BLK, 2 * BLK)
    dil_biases = [block_rel_bias(rel_bias[:, SWA_HEADS:], r)[:, None] for _, r in DIL_PAIRS]
    v_first = None
    for layer in range(DEPTH):
        h = rms_norm(x, norm_mix[layer])
        proj = h @ w_in[layer]
        c_b, c_c, c_u, s_q, s_k, s_v, d_q, d_k, d_v, rw = split_cols(proj, IN_SPLITS)
        y_conv = short_conv_mixer(c_b, c_c, c_u, conv_w[layer])
        y_swa = swa_mixer(s_q, s_k, s_v, swa_q_norm[layer], swa_k_norm[layer], swa_sink[layer], swa_bias)
        y_dil = dilated_mixer(d_q, d_k, d_v, dil_q_norm[layer], dil_k_norm[layer], dil_biases)
        vres = None if layer == 0 else (vres_v0[layer - 1], vres_v1[layer - 1], vres_v2[layer - 1])
        y_rwkv, v_first = rwkv7_mixer(rw, rwkv_mu[layer], decay_w0[layer], decay_w2[layer], iclr_a0[layer],
                                      iclr_a2[layer], gate_g2[layer], k_k[layer], k_a[layer], r_k[layer],
                                      ln_x_g[layer], ln_x_b[layer], v_first, vres)
        x = x + jnp.concatenate([y_conv, y_swa, y_dil, y_rwkv], axis=-1) @ w_out[layer]
        h = rms_norm(x, norm_ffn[layer])
        x = x + jnp.square(jax.nn.relu(h @ w_up[layer])) @ w_down[layer]
    return x
```

```python
import contextlib
import math
import numpy as np
import ml_dtypes
import concourse.bass as bass
import concourse.mybir as mybir
from concourse.bass_utils import run_bass_kernel_spmd

F32 = mybir.dt.float32
BF16 = mybir.dt.bfloat16
AF = mybir.ActivationFunctionType
ALU = mybir.AluOpType
AX = mybir.AxisListType

D = 2048
T = 2048
NT = 1024
DFF = 8192
NCORES = 8
RMS_EPS = 1e-6

STRICT_SAME = True
N_DMA_SEMS = 6


class Buf:
    __slots__ = ("name", "w", "r")

    def __init__(self, name=""):
        self.name = name
        self.w = None
        self.r = {}


class Op:
    __slots__ = ("eng", "fn", "deps", "observed", "val", "grp", "idx")

    def __init__(self, eng, fn):
        self.eng = eng
        self.fn = fn
        self.deps = []
        self.observed = False
        self.val = None
        self.grp = None
        self.idx = None


class Prog:
    ENGS = ("pe", "act", "dve", "pool", "sp")

    def __init__(self, nc):
        self.nc = nc
        self.ops = {e: [] for e in self.ENGS}
        self.n_ops = 0
        self.stack = contextlib.ExitStack()
        self.dma_cnt = {}
        self.dma_last = {}
        self.dma_rr = {e: 0 for e in self.ENGS}
        self._n = 0

    def sbuf(self, shape, dtype, name=None):
        self._n += 1
        return self.stack.enter_context(self.nc.sbuf_tensor(name or f"sb{self._n}", list(shape), dtype))

    def psum(self, shape, dtype, name=None):
        self._n += 1
        return self.stack.enter_context(self.nc.psum_tensor(name or f"ps{self._n}", list(shape), dtype))

    def _track(self, op, reads, writes, deps):
        ds = set()
        for b in reads:
            if b.w is not None:
                ds.add(b.w)
        for b in writes:
            if b.w is not None:
                ds.add(b.w)
            for r in b.r.values():
                ds.add(r)
        for d in deps:
            if d is not None:
                ds.add(d)
        ds.discard(op)
        for d in ds:
            if d.grp is None and op.grp is None and d.eng == op.eng:
                if d.eng == "pe" or not STRICT_SAME:
                    continue
            d.observed = True
            op.deps.append(d)
        key = op.eng if op.grp is None else ("dma", op.eng)
        for b in reads:
            b.r[key] = op
        for b in writes:
            b.w = op
            b.r = {}

    def op(self, eng, fn, reads=(), writes=(), deps=()):
        o = Op(eng, fn)
        o.idx = self.n_ops
        self.n_ops += 1
        self._track(o, reads, writes, deps)
        self.ops[eng].append(o)
        return o

    def dma(self, queue, out, in_, reads=(), writes=(), deps=()):
        o = Op(queue, lambda e: e.dma_start(out=out, in_=in_))
        o.idx = self.n_ops
        self.n_ops += 1
        sem_i = self.dma_rr[queue] % N_DMA_SEMS
        self.dma_rr[queue] += 1
        k = (queue, sem_i)
        self.dma_cnt[k] = self.dma_cnt.get(k, 0) + 1
        o.grp = (queue, sem_i, 16 * self.dma_cnt[k])
        prev = self.dma_last.get(k)
        self.dma_last[k] = o
        self._track(o, reads, writes, deps)
        if prev is not None and prev not in o.deps:
            prev.observed = True
            o.deps.append(prev)
        self.ops[queue].append(o)
        return o

    def wait_all(self, eng, ops):
        o = Op(eng, None)
        o.idx = self.n_ops
        self.n_ops += 1
        for d in ops:
            d.observed = True
            o.deps.append(d)
        self.ops[eng].append(o)
        return o

    def barrier(self):
        lasts = []
        for e in self.ENGS:
            for o in reversed(self.ops[e]):
                if o.grp is None and o.fn is not None:
                    lasts.append(o)
                    break
        lasts.extend(self.dma_last.values())
        for e in self.ENGS:
            if self.ops[e]:
                self.wait_all(e, lasts)

    def emit(self):
        nc = self.nc
        st = self.stack
        esem = {e: st.enter_context(nc.semaphore(f"s_{e}")) for e in self.ENGS}
        dsem = {}
        for e in self.ENGS:
            for i in range(min(N_DMA_SEMS, self.dma_rr[e])):
                dsem[(e, i)] = st.enter_context(nc.semaphore(f"d_{e}{i}"))
        for e in self.ENGS:
            c = 0
            for o in self.ops[e]:
                if o.grp is None and o.observed and o.fn is not None:
                    c += 1
                    o.val = c

        def token(d):
            if d.grp is not None:
                return dsem[(d.grp[0], d.grp[1])], d.grp[2], ("d", d.grp[0], d.grp[1])
            return esem[d.eng], d.val, ("e", d.eng)

        block = st.enter_context(nc.Block())

        def run(ename):
            def body(eng):
                waited = {}
                for o in self.ops[ename]:
                    for d in sorted(o.deps, key=lambda x: x.idx):
                        sem, val, key = token(d)
                        if waited.get(key, 0) >= val:
                            continue
                        eng.wait_ge(sem, val)
                        waited[key] = val
                    if o.fn is None:
                        continue
                    inst = o.fn(eng)
                    if o.grp is not None:
                        inst.then_inc(dsem[(o.grp[0], o.grp[1])], 16)
                    elif o.observed:
                        inst.then_inc(esem[ename], 1)
            return body

        for ename, reg in (("pe", block.tensor), ("act", block.scalar), ("dve", block.vector),
                           ("pool", block.gpsimd), ("sp", block.sync)):
            if self.ops[ename]:
                reg(run(ename))

    def close(self):
        self.stack.close()


class K:
    def __init__(self, P):
        self.P = P
        self.banks = []
        self.bank_bufs = []
        self.bank_i = 0

    def init_psum(self, n=8):
        for i in range(n):
            self.banks.append(self.P.psum([128, 512], F32, name=f"bank{i}"))
            self.bank_bufs.append(Buf(f"bank{i}"))

    def bank(self):
        i = self.bank_i % len(self.banks)
        self.bank_i += 1
        return self.banks[i], self.bank_bufs[i]

    def mm(self, out, lhsT, rhs, start=True, stop=True, r=(), w=(), deps=()):
        return self.P.op("pe", lambda e: e.matmul(out, lhsT, rhs, start=start, stop=stop), r, w, deps)

    def transpose(self, out, in_, ident, r=(), w=()):
        return self.P.op("pe", lambda e: e.transpose(out, in_, ident), r, w)

    def act(self, out, in_, func, r=(), w=(), bias=None, scale=None, eng="act"):
        kw = {}
        if bias is not None:
            kw["bias"] = bias
        if scale is not None:
            kw["scale"] = scale
        return self.P.op(eng, lambda e: e.activation(out, in_, func, **kw), r, w)

    def tt(self, out, in0, in1, op, r=(), w=(), eng="dve"):
        return self.P.op(eng, lambda e: e.tensor_tensor(out, in0, in1, op), r, w)

    def ts(self, out, in0, s1, op0, s2=None, op1=None, r=(), w=(), eng="dve"):
        if op1 is None:
            return self.P.op(eng, lambda e: e.tensor_scalar(out, in0, s1, None, op0), r, w)
        return self.P.op(eng, lambda e: e.tensor_scalar(out, in0, s1, s2, op0, op1), r, w)

    def stt(self, out, in0, scalar, in1, op0, op1, r=(), w=(), eng="dve"):
        return self.P.op(eng, lambda e: e.scalar_tensor_tensor(out, in0, scalar, in1, op0, op1), r, w)

    def copy(self, out, in_, r=(), w=(), eng="dve"):
        if eng == "act":
            return self.P.op("act", lambda e: e.activation(out, in_, AF.Copy), r, w)
        return self.P.op(eng, lambda e: e.tensor_copy(out, in_), r, w)

    def recip(self, out, in_, r=(), w=()):
        return self.P.op("dve", lambda e: e.reciprocal(out, in_), r, w)

    def memset(self, ap, val, w=(), eng="pool"):
        return self.P.op(eng, lambda e: e.memset(ap, val), (), w)

    def dma(self, out, in_, r=(), w=(), q="sp", deps=()):
        return self.P.dma(q, out, in_, r, w, deps)


def bufs2(n, m, name):
    return [[Buf(f"{name}{i}_{j}") for j in range(m)] for i in range(n)]


def emit_rmsnorm(k, xs, xb, gcol, gbuf, hs, hb, ones_bf, ones_buf, sq_t, sq_b, rs_t, rs_b, ntok, nchunk=16, dim=D):
    ng = ntok // 512
    for g in range(ng):
        sl = slice(g * 512, (g + 1) * 512)
        ps, pb = k.bank()
        for c in range(nchunk):
            sq, sb = sq_t[c % len(sq_t)], sq_b[c % len(sq_t)]
            k.act(sq[:], xs[:, c, sl], AF.Square, r=[xb[c][g]], w=[sb])
            k.mm(ps[:], ones_bf[:], sq[:], start=(c == 0), stop=(c == nchunk - 1), r=[sb, ones_buf], w=[pb])
        rs, rb = rs_t[g % len(rs_t)], rs_b[g % len(rs_t)]
        k.act(rs[:], ps[:], AF.Sqrt, r=[pb], w=[rb], bias=RMS_EPS, scale=1.0 / dim)
        k.recip(rs[:], rs[:], r=[rb], w=[rb])
        for c in range(nchunk):
            k.stt(hs[:, c, sl], xs[:, c, sl], gcol[:, c:c + 1], rs[:], ALU.mult, ALU.mult,
                  r=[xb[c][g], gbuf, rb], w=[hb[c][g]])


def build_dense(with_next_norm, only_norm=False):
    nc = bass.Bass("TRN2", target_bir_lowering=False)
    dt = lambda name, shape, dtype, kind: nc.dram_tensor(name, shape, dtype, kind=kind).ap()
    xT = dt("xT", [D, NT], F32, "ExternalInput")
    if not only_norm:
        yT = dt("yT", [D, NT], BF16, "ExternalInput")
        w_out = dt("w_out", [D, D], F32, "ExternalInput")
        w_up = dt("w_up", [D, DFF], F32, "ExternalInput")
        w_down = dt("w_down", [DFF, D], F32, "ExternalInput")
        g_ffn = dt("g_ffn", [128, 16], F32, "ExternalInput")
        x_out = dt("x_out", [D, NT], F32, "ExternalOutput")
    if with_next_norm or only_norm:
        g_mix = dt("g_mix", [128, 16], F32, "ExternalInput")
        h_out = dt("h_out", [D, NT], BF16, "ExternalOutput")

    P = Prog(nc)
    k = K(P)
    k.init_psum(8)
    NG = NT // 512
    xs = P.sbuf([128, 16, NT], F32, "xs")
    ys = P.sbuf([128, 16, NT], BF16, "ys")
    xb = bufs2(16, NG, "x")
    yb = bufs2(16, NG, "y")
    ones_bf = P.sbuf([128, 128], BF16, "ones")
    ones_buf = Buf("ones")
    k.memset(ones_bf[:], 1.0, w=[ones_buf])
    sq_t = [P.sbuf([128, 512], BF16, f"sq{i}") for i in range(3)]
    sq_b = [Buf(f"sq{i}") for i in range(3)]
    rs_t = [P.sbuf([128, 512], F32, f"rs{i}") for i in range(2)]
    rs_b = [Buf(f"rs{i}") for i in range(2)]
    outs = []

    xv = xT.rearrange("(c p) t -> p c t", p=128)
    for c in range(16):
        k.dma(xs[:, c, :], xv[:, c, :], w=xb[c])

    if not only_norm:
        gf = P.sbuf([128, 16], F32, "gf")
        gfb = Buf("gf")
        k.dma(gf[:], g_ffn, w=[gfb])
        yv = yT.rearrange("(c p) t -> p c t", p=128)
        for c in range(16):
            k.dma(ys[:, c, :], yv[:, c, :], w=yb[c])
        wA = [P.sbuf([128, 16, 512], BF16, f"wA{i}") for i in range(2)]
        wAb = [[Buf(f"wA{i}_{j}") for j in range(4)] for i in range(2)]
        wB = [P.sbuf([128, 4, 2048], BF16, f"wB{i}") for i in range(2)]
        wBb = [[Buf(f"wB{i}_{j}") for j in range(4)] for i in range(2)]
        act = [P.sbuf([128, 4, NT], BF16, f"act{i}") for i in range(2)]
        actb = [[[Buf(f"act{i}_{j}_{g}") for g in range(NG)] for j in range(4)] for i in range(2)]
        tmp = [P.sbuf([128, 512], F32, f"tmp{i}") for i in range(3)]
        tmpb = [Buf(f"tmp{i}") for i in range(3)]
        nA = [0]

        def load_A(src, col0):
            i = nA[0] % 2
            nA[0] += 1
            v = src.rearrange("(c p) n -> p c n", p=128)
            for j in range(4):
                k.dma(wA[i][:, 4 * j:4 * j + 4, :], v[:, 4 * j:4 * j + 4, col0:col0 + 512], w=[wAb[i][j]], q="pool")
            return i

        for blk in range(4):
            i = load_A(w_out, blk * 512)
            for dcl in range(4):
                dc = blk * 4 + dcl
                for g in range(NG):
                    sl = slice(g * 512, (g + 1) * 512)
                    ps, pb = k.bank()
                    for fc in range(16):
                        k.mm(ps[:], wA[i][:, fc, dcl * 128:(dcl + 1) * 128], ys[:, fc, sl], start=(fc == 0), stop=(fc == 15),
                             r=[wAb[i][fc // 4], yb[fc][g]], w=[pb])
                    k.tt(xs[:, dc, sl], xs[:, dc, sl], ps[:], ALU.add, r=[pb, xb[dc][g]], w=[xb[dc][g]])

        emit_rmsnorm(k, xs, xb, gf, gfb, ys, yb, ones_bf, ones_buf, sq_t, sq_b, rs_t, rs_b, NT)

        NFG = DFF // 512
        wdv = w_down.rearrange("(f c p) d -> f p c d", p=128, c=4)

        def emit_up(fg):
            ia = load_A(w_up, fg * 512)
            ib = fg % 2
            for j in range(4):
                k.dma(wB[ib][:, j, :], wdv[fg][:, j, :], w=[wBb[ib][j]], q="pool")
            for fcl in range(4):
                for g in range(NG):
                    sl = slice(g * 512, (g + 1) * 512)
                    ps, pb = k.bank()
                    for kc in range(16):
                        k.mm(ps[:], wA[ia][:, kc, fcl * 128:(fcl + 1) * 128], ys[:, kc, sl], start=(kc == 0), stop=(kc == 15),
                             r=[wAb[ia][kc // 4], yb[kc][g]], w=[pb])
                    ti = (fcl * NG + g) % 3
                    k.act(tmp[ti][:], ps[:], AF.Relu, r=[pb], w=[tmpb[ti]])
                    k.tt(act[ib][:, fcl, sl], tmp[ti][:], tmp[ti][:], ALU.mult, r=[tmpb[ti]], w=[actb[ib][fcl][g]])

        def emit_down(fg):
            ib = fg % 2
            for dc in range(16):
                for g in range(NG):
                    sl = slice(g * 512, (g + 1) * 512)
                    ps, pb = k.bank()
                    for fcl in range(4):
                        k.mm(ps[:], wB[ib][:, fcl, dc * 128:(dc + 1) * 128], act[ib][:, fcl, sl], start=(fcl == 0), stop=(fcl == 3),
                             r=[wBb[ib][fcl], actb[ib][fcl][g]], w=[pb])
                    k.tt(xs[:, dc, sl], xs[:, dc, sl], ps[:], ALU.add, r=[pb, xb[dc][g]], w=[xb[dc][g]])

        emit_up(0)
        for fg in range(NFG):
            if fg + 1 < NFG:
                emit_up(fg + 1)
            emit_down(fg)

        xov = x_out.rearrange("(c p) t -> p c t", p=128)
        for c in range(16):
            outs.append(k.dma(xov[:, c, :], xs[:, c, :], r=xb[c]))

    if with_next_norm or only_norm:
        gm = P.sbuf([128, 16], F32, "gm")
        gmb = Buf("gm")
        k.dma(gm[:], g_mix, w=[gmb])
        emit_rmsnorm(k, xs, xb, gm, gmb, ys, yb, ones_bf, ones_buf, sq_t, sq_b, rs_t, rs_b, NT)
        hov = h_out.rearrange("(c p) t -> p c t", p=128)
        for c in range(16):
            outs.append(k.dma(hov[:, c, :], ys[:, c, :], r=yb[c]))

    P.wait_all("sp", outs)
    P.emit()
    P.close()
    return nc


def _t5_bucket_np(dist):
    dist = np.maximum(dist, 0)
    scaled = np.log(np.maximum(dist, 1).astype(np.float32) / np.float32(16)) / np.float32(math.log(8.0))
    large = 16 + (scaled * np.float32(16)).astype(np.int32)
    large = np.minimum(large, 31)
    return np.where(dist < 16, dist, large)


def attn_consts():
    oh = np.zeros((32, 4, 384), np.float32)
    mad = np.full((8, 4, 384), -30000.0, np.float32)
    for ty, (stride, maxd) in enumerate(((1, 127), (1, 128), (4, 128), (16, 128))):
        for m in range(384):
            dist = m - 127
            if 0 <= dist <= maxd:
                b = int(_t5_bucket_np(np.array([dist * stride]))[0])
                oh[b, ty, m] = 1.0
                mad[:, ty, m] = 0.0
    J = np.zeros((128, 128), np.float32)
    J[np.arange(128), 127 - np.arange(128)] = 1.0
    ident = np.eye(128, dtype=np.float32).astype(ml_dtypes.bfloat16)
    ob = np.zeros((128, 128), np.float32)
    ob[:64, :64] = 1.0
    ob[64:, 64:] = 1.0
    return {"oh": oh, "maskadd": mad, "cJ": J, "ident": ident, "onesblk": ob.astype(ml_dtypes.bfloat16)}


def build_attn():
    nc = bass.Bass("TRN2", target_bir_lowering=False)
    dt = lambda name, shape, dtype, kind="ExternalInput": nc.dram_tensor(name, shape, dtype, kind=kind).ap()
    hT = dt("hT", [D, T], BF16)
    w_a = dt("w_a", [D, 2048], F32)
    convw = dt("convw", [128, 2, 3], F32)
    gains = dt("gains", [128, 4], F32)
    sinkc = dt("sinkc", [128, 2], F32)
    relb = dt("relb", [32, 8], F32)
    oh_d = dt("oh", [32, 4, 384], F32)
    mad_d = dt("maskadd", [8, 4, 384], F32)
    cJ_d = dt("cJ", [128, 128], F32)
    ident_d = dt("ident", [128, 128], BF16)
    onesblk_d = dt("onesblk", [128, 128], BF16)
    yT = dt("yT_a", [768, T], BF16, "ExternalOutput")
    Gd_t = nc.dram_tensor("Gd", [8, 4, 384], F32, kind="Internal")
    Gd = Gd_t.ap()

    P = Prog(nc)
    k = K(P)
    k.init_psum(8)
    NTG = T // 512

    def load_small(name, src, shape, dtype=F32):
        t = P.sbuf(shape, dtype, name)
        b = Buf(name)
        k.dma(t[:], src, w=[b])
        return t, b

    scr = P.sbuf([128, 8208], F32, "scr")

    def alias_small(name, src, view):
        b = Buf(name)
        k.dma(view, src, w=[b])
        return view, b

    convw_s, convw_b = load_small("convw_s", convw, [128, 2, 3])
    gains_s, gains_b = load_small("gains_s", gains, [128, 4])
    sink_s, sink_b = load_small("sink_s", sinkc, [128, 2])
    oh_s, oh_b = alias_small("oh_s", oh_d, scr[0:32, 0:1536].rearrange("p (a b) -> p a b", b=384))
    mad_s, mad_b = alias_small("mad_s", mad_d, scr[0:8, 1536:3072].rearrange("p (a b) -> p a b", b=384))
    cJ_s, cJ_b = alias_small("cJ_s", cJ_d, scr[:, 5120:5248])
    relb_s, relb_b = alias_small("relb_s", relb, scr[0:32, 5248:5256])
    ident_s, ident_b = load_small("ident_s", ident_d, [128, 128], BF16)
    onesblk_s, onesblk_b = load_small("onesblk_s", onesblk_d, [128, 128], BF16)
    ones_bf = P.sbuf([128, 128], BF16, "ones_bf")
    ones_b = Buf("ones_bf")
    k.memset(ones_bf[:], 1.0, w=[ones_b])
    zeros_bf = P.sbuf([128, 64], BF16, "zeros_bf")
    zeros_b = Buf("zeros_bf")
    k.memset(zeros_bf[:], 0.0, w=[zeros_b])
    esink = P.sbuf([128, 2], F32, "esink")
    esink_b = Buf("esink")
    k.act(esink[:], sink_s[:], AF.Exp, r=[sink_b], w=[esink_b])

    hs = P.sbuf([128, 16, T], BF16, "hs")
    hb = [Buf(f"h{c}") for c in range(16)]
    hv = hT.rearrange("(c p) t -> p c t", p=128)
    for c in range(16):
        k.dma(hs[:, c, :], hv[:, c, :], w=[hb[c]])

    Gs = scr[0:8, 3072:4608].rearrange("p (a b) -> p a b", b=384)
    Gs_b = Buf("Gs")
    for ty in range(4):
        ps, pb = k.bank()
        k.mm(ps[0:8, 0:384], relb_s, oh_s[:, ty, :], r=[relb_b, oh_b], w=[pb])
        k.tt(Gs[:, ty, :], ps[0:8, 0:384], mad_s[:, ty, :], ALU.add, r=[pb, mad_b], w=[Gs_b])
    Gd_b = Buf("Gd")
    k.dma(Gd, Gs, r=[Gs_b], w=[Gd_b])
    BM = [[P.sbuf([128, 2, 256], F32, f"BM{ty}_{c}") for c in range(2)] for ty in range(3)]
    BM16 = [P.sbuf([128, 2, 128], F32, f"BM16_{c}") for c in range(2)]
    BMb = [[Buf(f"BM{ty}_{c}") for c in range(2)] for ty in range(4)]
    Ut = [scr[:, 4608 + 256 * i:4864 + 256 * i] for i in range(2)]
    Ub = [Buf(f"U{i}") for i in range(2)]
    ui = 0
    for ty in range(4):
        for c in range(2):
            for h in range(2):
                col = (0 if ty == 0 else 4) + 2 * c + h
                src = bass.AP(Gd_t, (col * 4 + ty) * 384, [[1, 128], [1, 256]])
                u, ub = Ut[ui % 2], Ub[ui % 2]
                ui += 1
                k.dma(u, src, r=[Gd_b], w=[ub])
                ps, pb = k.bank()
                k.mm(ps[:, 0:256], cJ_s, u, r=[cJ_b, ub], w=[pb])
                if ty < 3:
                    k.copy(BM[ty][c][:, h, 0:128], ps[:, 128:256], r=[pb], w=[BMb[ty][c]])
                    k.copy(BM[ty][c][:, h, 128:256], ps[:, 0:128], r=[pb], w=[BMb[ty][c]], eng="act")
                else:
                    k.copy(BM16[c][:, h, :], ps[:, 0:128], r=[pb], w=[BMb[ty][c]])

    P.barrier()
    cb_t = scr[:, 0:2048]
    cc_t = scr[:, 2048:4096]
    z_t = scr[:, 4096:6146]
    acc_t = scr[:, 6146:8194]
    cb_b = [Buf(f"cb{g}") for g in range(NTG)]
    cc_b = [Buf(f"cc{g}") for g in range(NTG)]
    z_b = [Buf(f"z{g}") for g in range(NTG)]
    zpad_b = Buf("zpad")
    acc_b = [Buf(f"acc{g}") for g in range(NTG)]
    k.memset(z_t[:, 0:2], 0.0, w=[zpad_b])
    Vr_all = scr[:, 0:6144].bitcast(BF16)
    Vr = [Vr_all[:, ri * 4096:(ri + 1) * 4096].rearrange("p (t f) -> p t f", f=256) for ri in range(3)]
    Vr_b = [[[Buf(f"Vr{ri}_{q}_{c}") for c in range(2)] for q in range(4)] for ri in range(3)]

    QTs = P.sbuf([128, 2, T], BF16, "QTs")
    QTs_b = bufs2(2, NTG, "QTs")
    KT2 = P.sbuf([128, T], BF16, "KT2")
    KT2_b = [Buf(f"KT2_{g}") for g in range(NTG)]
    VT2 = P.sbuf([128, T], BF16, "VT2")
    VT2_b = [Buf(f"VT2_{g}") for g in range(NTG)]
    Vs = P.sbuf([128, 16, 64], BF16, "Vs")
    Vs_b = [Buf(f"Vs{q}") for q in range(4)]
    QTd = P.sbuf([128, 2, T], BF16, "QTd")
    QTd_b = bufs2(2, NTG, "QTd")
    KTd = P.sbuf([128, 2, T], BF16, "KTd")
    KTd_b = bufs2(2, NTG, "KTd")
    VTd = P.sbuf([128, 2, T], BF16, "VTd")
    VTd_b = bufs2(2, NTG, "VTd")
    sq_t = [P.sbuf([128, 512], BF16, f"sq{i}") for i in range(2)]
    sq_b = [Buf(f"sq{i}") for i in range(2)]
    rs_t = [P.sbuf([128, 512], F32, f"rs{i}") for i in range(2)]
    rs_b = [Buf(f"rs{i}") for i in range(2)]
    ssb_t = [P.sbuf([128, 2, 256], F32, f"ssb{i}") for i in range(2)]
    ssb_b = [Buf(f"ssb{i}") for i in range(2)]
    pt_t = [P.sbuf([128, 2, 256], BF16, f"pt{i}") for i in range(3)]
    pt_b = [Buf(f"pt{i}") for i in range(3)]
    yt_t = [P.sbuf([128, T], BF16, f"yt{i}") for i in range(3)]
    yt_b = [[Buf(f"yt{i}_{g}") for g in range(NTG)] for i in range(3)]
    wblk = [P.sbuf([128, 16, 256], BF16, f"wblk{i}") for i in range(2)]
    wblk_b = [[Buf(f"wblk{i}_{j}") for j in range(2)] for i in range(2)]
    cnt = {"sq": 0, "rs": 0, "ssb": 0, "pt": 0, "yt": 0}
    outs = []

    def nxt(name, tiles, bufs):
        i = cnt[name] % len(tiles)
        cnt[name] += 1
        return tiles[i], bufs[i]

    def norm_consume(ps, pb, gcol, out_ap, out_bufs):
        sq, sb = nxt("sq", sq_t, sq_b)
        k.act(sq[:], ps[:], AF.Square, r=[pb], w=[sb])
        ps2, pb2 = k.bank()
        k.mm(ps2[:], onesblk_s[:], sq[:], r=[sb, onesblk_b], w=[pb2])
        rs, rb = nxt("rs", rs_t, rs_b)
        k.act(rs[:], ps2[:], AF.Sqrt, r=[pb2], w=[rb], bias=RMS_EPS, scale=1.0 / 64)
        k.recip(rs[:], rs[:], r=[rb], w=[rb])
        k.stt(out_ap, ps[:], gains_s[:, gcol:gcol + 1], rs[:], ALU.mult, ALU.mult, r=[pb, rb, gains_b], w=out_bufs)

    def store_y(yi, row0):
        for g in range(NTG):
            sl = slice(g * 512, (g + 1) * 512)
            outs.append(k.dma(yT[row0:row0 + 128, sl], yt_t[yi][:, sl], r=[yt_b[yi][g]]))

    state = {}

    def consume(ch, g, ps, pb):
        sl = slice(g * 512, (g + 1) * 512)
        if ch < 6:
            j, kind = ch // 3, ch % 3
            if kind == 0:
                k.copy(cb_t[:, sl], ps[:], r=[pb], w=[cb_b[g]], eng="act")
            elif kind == 1:
                k.copy(cc_t[:, sl], ps[:], r=[pb], w=[cc_b[g]], eng="act")
            else:
                if g == 0:
                    state["yi"] = cnt["yt"] % 3
                    cnt["yt"] += 1
                yi = state["yi"]
                k.tt(z_t[:, 2 + g * 512:2 + (g + 1) * 512], ps[:], cc_t[:, sl], ALU.mult, r=[pb, cc_b[g]], w=[z_b[g]])
                zr = [z_b[g], zpad_b] + ([z_b[g - 1]] if g > 0 else [])
                k.ts(acc_t[:, sl], z_t[:, 2 + g * 512:2 + (g + 1) * 512], convw_s[:, j, 2:3], ALU.mult, r=zr + [convw_b], w=[acc_b[g]])
                k.stt(acc_t[:, sl], z_t[:, 1 + g * 512:1 + (g + 1) * 512], convw_s[:, j, 1:2], acc_t[:, sl], ALU.mult, ALU.add,
                      r=zr + [convw_b, acc_b[g]], w=[acc_b[g]])
                k.stt(acc_t[:, sl], z_t[:, g * 512:(g + 1) * 512], convw_s[:, j, 0:1], acc_t[:, sl], ALU.mult, ALU.add,
                      r=zr + [convw_b, acc_b[g]], w=[acc_b[g]])
                o = k.tt(yt_t[yi][:, sl], acc_t[:, sl], cb_t[:, sl], ALU.mult, r=[acc_b[g], cb_b[g]], w=[yt_b[yi][g]])
                state["conv_last"] = o
                if g == NTG - 1:
                    store_y(yi, j * 128)
        elif ch < 8:
            c = ch - 6
            norm_consume(ps, pb, 0, QTs[:, c, sl], [QTs_b[c][g]])
        elif ch == 8:
            norm_consume(ps, pb, 1, KT2[:, sl], [KT2_b[g]])
        elif ch == 9:
            k.copy(VT2[:, sl], ps[:], r=[pb], w=[VT2_b[g]], eng="act")
        elif ch < 12:
            c = ch - 10
            norm_consume(ps, pb, 2, QTd[:, c, sl], [QTd_b[c][g]])
        elif ch < 14:
            c = ch - 12
            norm_consume(ps, pb, 3, KTd[:, c, sl], [KTd_b[c][g]])
        else:
            c = ch - 14
            k.copy(VTd[:, c, sl], ps[:], r=[pb], w=[VTd_b[c][g]], eng="act")

    wv = w_a.rearrange("(c p) n -> p c n", p=128)
    for blk in range(8):
        wi = blk % 2
        for j in range(2):
            k.dma(wblk[wi][:, 8 * j:8 * j + 8, :], wv[:, 8 * j:8 * j + 8, blk * 256:(blk + 1) * 256], w=[wblk_b[wi][j]], q="pool")
        for cl in range(2):
            ch = blk * 2 + cl
            for g in range(NTG):
                sl = slice(g * 512, (g + 1) * 512)
                ps, pb = k.bank()
                for kc in range(16):
                    k.mm(ps[:], wblk[wi][:, kc, cl * 128:(cl + 1) * 128], hs[:, kc, sl], start=(kc == 0), stop=(kc == 15),
                         r=[wblk_b[wi][kc // 8], hb[kc]], w=[pb])
                consume(ch, g, ps, pb)

    for q in range(4):
        ps, pb = k.bank()
        psb = ps[:].bitcast(BF16)
        for jj in range(4):
            j = q * 4 + jj
            k.transpose(psb[:, jj * 128:(jj + 1) * 128], VT2[:, j * 128:(j + 1) * 128], ident_s[:],
                        r=[VT2_b[j // 4], ident_b], w=[pb])
        k.copy(Vs[:, 4 * q:4 * q + 4, :], psb[:, 0:512].rearrange("p (j f) -> p j f", f=128)[:, :, 0:64], r=[pb], w=[Vs_b[q]])
    for bq in Vr_b:
        for bb in bq:
            for b in bb:
                b.w = state["conv_last"]
    for ri, r_ in enumerate((1, 4, 16)):
        for c in range(2):
            for q in range(4):
                ps, pb = k.bank()
                psb = ps[:].bitcast(BF16)
                for jj in range(4):
                    j = q * 4 + jj
                    if r_ == 1:
                        src = VTd[:, c, j * 128:(j + 1) * 128]
                        rb = [VTd_b[c][j // 4]]
                    elif r_ == 4:
                        cls, blk = j // 4, j % 4
                        src = VTd[:, c, cls + 512 * blk:512 * (blk + 1):4]
                        rb = [VTd_b[c][blk]]
                    else:
                        src = VTd[:, c, j:T:16]
                        rb = VTd_b[c]
                    k.transpose(psb[:, jj * 128:(jj + 1) * 128], src, ident_s[:], r=rb + [ident_b], w=[pb])
                k.copy(Vr[ri][:, 4 * q:4 * q + 4, c * 128:(c + 1) * 128], psb[:, 0:512].rearrange("p (j f) -> p j f", f=128),
                       r=[pb], w=[Vr_b[ri][q][c]], eng=("act" if q % 2 else "dve"))

    banks, bb = k.banks, k.bank_bufs

    def score_block(sbank, KT_of, QT_of, kb_prev, kb_cur, qb, has_prev, BMt, BMbuf, nq=128):
        for h in range(2):
            hp = slice(h * 64, (h + 1) * 64)
            ps, pb = banks[sbank + h], bb[sbank + h]
            if has_prev:
                k.mm(ps[:, 0:nq], KT_of(hp, 0), QT_of(hp), r=kb_prev + qb, w=[pb])
            k.mm(ps[:, 128:128 + nq], KT_of(hp, 1), QT_of(hp), r=kb_cur + qb, w=[pb])
        lo = 0 if has_prev else 128
        ssb, ssbb = nxt("ssb", ssb_t, ssb_b)
        pt, ptb = nxt("pt", pt_t, pt_b)
        for h in range(2):
            k.stt(ssb[:, h, lo:256], banks[sbank + h][:, lo:256], 0.125, BMt[:, h, lo:256], ALU.mult, ALU.add,
                  r=[bb[sbank + h], BMbuf], w=[ssbb])
        k.act(pt[:, :, lo:256], ssb[:, :, lo:256], AF.Exp, r=[ssbb], w=[ptb])
        return pt, ptb

    def pv_block(numps, nb, denps, db, cols, V_of, vb_prev, vb_cur, pt, ptb, has_prev, first):
        for h in range(2):
            hp = slice(h * 64, (h + 1) * 64)
            if has_prev:
                k.mm(numps[hp, cols], V_of(h, 0), pt[:, h, 0:128], start=first, stop=False, r=[ptb] + vb_prev, w=[nb])
                k.mm(denps[hp, cols], ones_bf[:, 0:64], pt[:, h, 0:128], start=first, stop=False, r=[ptb, ones_b], w=[db])
            st = first and not has_prev
            k.mm(numps[hp, cols], V_of(h, 1), pt[:, h, 128:256], start=st, stop=True, r=[ptb] + vb_cur, w=[nb])
            k.mm(denps[hp, cols], ones_bf[:, 0:64], pt[:, h, 128:256], start=st, stop=True, r=[ptb, ones_b], w=[db])

    it = 0
    for c in range(2):
        yi = cnt["yt"] % 3
        cnt["yt"] += 1
        for n in range(4):
            numps, nb = banks[4 + 2 * (n % 2)], bb[4 + 2 * (n % 2)]
            denps, db = banks[5 + 2 * (n % 2)], bb[5 + 2 * (n % 2)]
            for i in range(4 * n, 4 * n + 4):
                qs = slice(i * 128, (i + 1) * 128)
                kp = slice((i - 1) * 128, i * 128)
                sbank = 2 * (it % 2)
                it += 1
                pt, ptb = score_block(
                    sbank,
                    lambda hp, w_, kp=kp, qs=qs: KT2[hp, kp] if w_ == 0 else KT2[hp, qs],
                    lambda hp, qs=qs, c=c: QTs[hp, c, qs],
                    [KT2_b[(i - 1) // 4]] if i > 0 else [], [KT2_b[i // 4]], [QTs_b[c][i // 4]], i > 0, BM[0][c], BMb[0][c])
                cols = slice((i % 4) * 128, (i % 4 + 1) * 128)
                pv_block(numps, nb, denps, db, cols,
                         lambda h, w_, i=i: Vs[:, i - 1, :] if w_ == 0 else Vs[:, i, :],
                         [Vs_b[(i - 1) // 4]] if i > 0 else [], [Vs_b[i // 4]], pt, ptb, i > 0, True)
            sl = slice(n * 512, (n + 1) * 512)
            rs, rb = nxt("rs", rs_t, rs_b)
            k.ts(rs[:], denps[:], esink[:, c:c + 1], ALU.add, r=[db, esink_b], w=[rb])
            k.recip(rs[:], rs[:], r=[rb], w=[rb])
            k.tt(yt_t[yi][:, sl], numps[:], rs[:], ALU.mult, r=[nb, rb], w=[yt_b[yi][n]])
        store_y(yi, 256 + c * 128)

    for c in range(2):
        yi = cnt["yt"] % 3
        cnt["yt"] += 1
        for n in range(4):
            numps, nb = banks[4 + 2 * (n % 2)], bb[4 + 2 * (n % 2)]
            denps, db = banks[5 + 2 * (n % 2)], bb[5 + 2 * (n % 2)]
            for h in range(2):
                hp = slice(h * 64, (h + 1) * 64)
                k.mm(numps[hp, :], zeros_bf[:, 0:64], hs[:, 0, 0:512], start=True, stop=False, r=[zeros_b, hb[0]], w=[nb])
                k.mm(denps[hp, :], zeros_bf[:, 0:64], hs[:, 0, 0:512], start=True, stop=False, r=[zeros_b, hb[0]], w=[db])
            for i in range(4 * n, 4 * n + 4):
                qs = slice(i * 128, (i + 1) * 128)
                kp = slice((i - 1) * 128, i * 128)
                sbank = 2 * (it % 2)
                it += 1
                pt, ptb = score_block(
                    sbank,
                    lambda hp, w_, kp=kp, qs=qs, c=c: KTd[hp, c, kp] if w_ == 0 else KTd[hp, c, qs],
                    lambda hp, qs=qs, c=c: QTd[hp, c, qs],
                    [KTd_b[c][(i - 1) // 4]] if i > 0 else [], [KTd_b[c][i // 4]], [QTd_b[c][i // 4]], i > 0, BM[1][c], BMb[1][c])
                cols = slice((i % 4) * 128, (i % 4 + 1) * 128)
                pv_block(numps, nb, denps, db, cols,
                         lambda h, w_, i=i, c=c: Vr[0][:, i - 1 if w_ == 0 else i, c * 128 + h * 64:c * 128 + (h + 1) * 64],
                         [Vr_b[0][(i - 1) // 4][c]] if i > 0 else [], [Vr_b[0][i // 4][c]], pt, ptb, i > 0, False)
            for cls in range(4):
                qs = slice(cls + 512 * n, 512 * (n + 1), 4)
                kp = slice(cls + 512 * (n - 1), 512 * n, 4)
                sbank = 2 * (it % 2)
                it += 1
                pt, ptb = score_block(
                    sbank,
                    lambda hp, w_, kp=kp, qs=qs, c=c: KTd[hp, c, kp] if w_ == 0 else KTd[hp, c, qs],
                    lambda hp, qs=qs, c=c: QTd[hp, c, qs],
                    [KTd_b[c][n - 1]] if n > 0 else [], [KTd_b[c][n]], [QTd_b[c][n]], n > 0, BM[2][c], BMb[2][c])
                cols = slice(cls, 512, 4)
                pv_block(numps, nb, denps, db, cols,
                         lambda h, w_, cls=cls, n=n, c=c: Vr[1][:, cls * 4 + (n - 1 if w_ == 0 else n), c * 128 + h * 64:c * 128 + (h + 1) * 64],
                         [Vr_b[1][cls][c]], [Vr_b[1][cls][c]], pt, ptb, n > 0, False)
            sbank = 2 * (it % 2)
            it += 1
            for h in range(2):
                hp = slice(h * 64, (h + 1) * 64)
                ps, pb = banks[sbank + h], bb[sbank + h]
                for cls in range(16):
                    k.mm(ps[:, cls * 32:(cls + 1) * 32], KTd[hp, c, cls:T:16], QTd[hp, c, cls + 512 * n:512 * (n + 1):16],
                         r=KTd_b[c] + [QTd_b[c][n]], w=[pb])
            ssb, ssbb = nxt("ssb", ssb_t, ssb_b)
            pt, ptb = nxt("pt", pt_t, pt_b)
            ssv = ssb[:].rearrange("p h x -> p (h x)")
            ptv = pt[:].rearrange("p h x -> p (h x)")
            ssb2, ssbb2 = nxt("ssb", ssb_t, ssb_b)
            pt2, ptb2 = nxt("pt", pt_t, pt_b)
            ssv2 = ssb2[:].rearrange("p h x -> p (h x)")
            ptv2 = pt2[:].rearrange("p h x -> p (h x)")
            for h, (sv, sbuf_, pv_, pbuf_) in enumerate(((ssv, ssbb, ptv, ptb), (ssv2, ssbb2, ptv2, ptb2))):
                bm = BM16[c][:, h, 32 * n:32 * (n + 1)].unsqueeze(1).broadcast_to([128, 16, 32])
                k.stt(sv.rearrange("p (a b) -> p a b", b=32), banks[sbank + h][:].rearrange("p (a b) -> p a b", b=32), 0.125, bm,
                      ALU.mult, ALU.add, r=[bb[sbank + h], BMb[3][c]], w=[sbuf_])
                k.act(pv_, sv, AF.Exp, r=[sbuf_], w=[pbuf_])
                hp = slice(h * 64, (h + 1) * 64)
                for cls in range(16):
                    cols = slice(cls, 512, 16)
                    last = (h == 1 and cls == 15)
                    k.mm(numps[hp, cols], Vr[2][:, cls, c * 128 + h * 64:c * 128 + (h + 1) * 64], pv_[:, cls * 32:(cls + 1) * 32],
                         start=False, stop=True, r=[pbuf_, Vr_b[2][cls // 4][c]], w=[nb])
                    k.mm(denps[hp, cols], ones_bf[:, 0:64], pv_[:, cls * 32:(cls + 1) * 32], start=False, stop=True,
                         r=[pbuf_, ones_b], w=[db])
            sl = slice(n * 512, (n + 1) * 512)
            rs, rb = nxt("rs", rs_t, rs_b)
            k.recip(rs[:], denps[:], r=[db], w=[rb])
            k.tt(yt_t[yi][:, sl], numps[:], rs[:], ALU.mult, r=[nb, rb], w=[yt_b[yi][n]])
        store_y(yi, 512 + c * 128)

    P.wait_all("sp", outs)
    P.emit()
    P.close()
    return nc


OFF = {"c_b": 0, "c_c": 512, "c_u": 1024, "s_q": 1536, "s_k": 2048, "s_v": 2176, "d_q": 2304, "d_k": 2816, "d_v": 3328,
       "r_r": 3840, "r_k": 4352, "r_v": 4864, "r_wd": 5376, "r_ad": 5440, "r_gd": 5504}


def gcol(g):
    return np.ascontiguousarray(g.reshape(16, 128).T)


def pack_attn_inputs(inp, l, hh, consts):
    w = inp["w_in"][l]
    cols = []
    for j in range(2):
        ch = 256 * hh + 128 * j
        for nm in ("c_b", "c_c", "c_u"):
            cols.append(w[:, OFF[nm] + ch:OFF[nm] + ch + 128])
    for c in range(2):
        cols.append(w[:, OFF["s_q"] + 256 * hh + 128 * c:OFF["s_q"] + 256 * hh + 128 * (c + 1)])
    sk = w[:, OFF["s_k"] + 64 * hh:OFF["s_k"] + 64 * hh + 64]
    sv = w[:, OFF["s_v"] + 64 * hh:OFF["s_v"] + 64 * hh + 64]
    cols += [sk, sk, sv, sv]
    for nm in ("d_q", "d_k", "d_v"):
        for c in range(2):
            cols.append(w[:, OFF[nm] + 256 * hh + 128 * c:OFF[nm] + 256 * hh + 128 * (c + 1)])
    w_a = np.ascontiguousarray(np.concatenate(cols, axis=1))
    assert w_a.shape == (D, 2048)
    cw = inp["conv_w"][l][:, 256 * hh:256 * hh + 256]
    convw = np.ascontiguousarray(cw.reshape(3, 2, 128).transpose(2, 1, 0))
    t64 = lambda v: np.concatenate([v, v])
    gains = np.ascontiguousarray(np.stack([t64(inp["swa_q_norm"][l]), t64(inp["swa_k_norm"][l]),
                                           t64(inp["dil_q_norm"][l]), t64(inp["dil_k_norm"][l])], axis=1))
    sk_ = inp["swa_sink"][l][4 * hh:4 * hh + 4]
    sinkc = np.ascontiguousarray(np.stack([np.repeat(sk_[0:2], 64), np.repeat(sk_[2:4], 64)], axis=1))
    rb = inp["rel_bias"]
    relb = np.ascontiguousarray(np.concatenate([rb[:, 4 * hh:4 * hh + 4], rb[:, 8 + 4 * hh:8 + 4 * hh + 4]], axis=1))
    d = {"w_a": w_a, "convw": convw, "gains": gains.astype(np.float32), "sinkc": sinkc.astype(np.float32), "relb": relb}
    d.update(consts)
    return d


SEG = 256
CH = 64
RW_DBG = {"stage": 99, "nseg": 8}
C0 = math.exp(-0.5)
LN_X_EPS = 64e-5


def rwkv_consts():
    ident = np.eye(128, dtype=np.float32)
    ob = np.zeros((128, 128), np.float32)
    ob[:64, :64] = 1.0
    ob[64:, 64:] = 1.0
    r = np.arange(64)
    mls = (r[None, :] < r[:, None]).astype(np.float32)
    mus = (r[None, :] > r[:, None]).astype(np.float32)
    mui = (r[None, :] >= r[:, None]).astype(np.float32)
    masks = np.ascontiguousarray(np.stack([mls, mus, mui], axis=1))
    return {"ident32": ident, "onesblk32": ob, "onesblk64": (ob / 64.0).astype(np.float32), "masks": masks}


def build_rwkv(layer1):
    nc = bass.Bass("TRN2", target_bir_lowering=False)
    dt = lambda name, shape, dtype, kind="ExternalInput": nc.dram_tensor(name, shape, dtype, kind=kind).ap()
    hT = dt("hT", [D, T], BF16)
    w_r = dt("w_r", [D, 1280], F32)
    cols_d = dt("rcols", [128, 32], F32)
    wa2_d = dt("wa2", [128, 256], F32)
    g2m_d = dt("g2m", [128, 256], F32)
    ident_d = dt("ident32", [128, 128], F32)
    ob32_d = dt("onesblk32", [128, 128], F32)
    ob64_d = dt("onesblk64", [128, 128], F32)
    masks_d = dt("masks", [64, 3, 64], F32)
    if layer1:
        v1_d = dt("v1p", [128, 4, 32], F32)
        v2m_d = dt("v2m", [32, 256], F32)
        vf_in = dt("vf_in", [256, T], F32)
    else:
        vf_out = dt("vf_out", [256, T], F32, "ExternalOutput")
    yT = dt("yT_r", [256, T], BF16, "ExternalOutput")

    P = Prog(nc)
    k = K(P)
    k.init_psum(8)
    NSEG = T // SEG
    NJ = SEG // CH

    def load_small(name, src, shape, dtype=F32):
        t = P.sbuf(shape, dtype, name)
        b = Buf(name)
        k.dma(t[:], src, w=[b])
        return t, b

    cols_s, cols_b = load_small("cols_s", cols_d, [128, 32])
    wa2_s, wa2_b = load_small("wa2_s", wa2_d, [128, 256])
    g2m_s, g2m_b = load_small("g2m_s", g2m_d, [128, 256])
    ident_s, ident_b = load_small("ident_s", ident_d, [128, 128])
    ob32_s, ob32_b = load_small("ob32_s", ob32_d, [128, 128])
    ob64_s, ob64_b = load_small("ob64_s", ob64_d, [128, 128])
    masks_s, masks_b = load_small("masks_s", masks_d, [64, 3, 64])
    if layer1:
        v1_s, v1_b = load_small("v1_s", v1_d, [128, 4, 32])
        v2m_s, v2m_b = load_small("v2m_s", v2m_d, [32, 256])
    MU, OMM, W0, A0, KK, KA, RK = 0, 10, 20, 22, 24, 26, 28
    cols2_d = dt("rcols2", [128, 8], F32)
    cols2_s, cols2_b = load_small("cols2_s", cols2_d, [128, 8])
    LNG, LNB, V0 = 0, 2, 4
    k.ts(cols_s[:, OMM:OMM + 10], cols_s[:, MU:MU + 10], -1.0, ALU.mult, 1.0, ALU.add, r=[cols_b], w=[cols_b])
    cmask = P.sbuf([128, SEG], F32, "cmask")
    cmask_b = Buf("cmask")
    k.memset(cmask[:], 1.0, w=[cmask_b])
    k.memset(cmask[:].rearrange("p (j t) -> p j t", t=CH)[:, :, 0:1], 0.0, w=[cmask_b])

    wr = P.sbuf([128, 16, 1280], BF16, "wr")
    wr_b = [[Buf(f"wr{i}_{j}") for j in range(2)] for i in range(5)]
    wv = w_r.rearrange("(c p) n -> p c n", p=128)
    for i in range(5):
        for j in range(2):
            k.dma(wr[:, 8 * j:8 * j + 8, 256 * i:256 * (i + 1)], wv[:, 8 * j:8 * j + 8, 256 * i:256 * (i + 1)], w=[wr_b[i][j]], q="pool")

    hseg = [P.sbuf([128, 16, SEG + 1], BF16, f"hseg{i}") for i in range(2)]
    hseg_b = [[Buf(f"hseg{i}_{q}") for q in range(4)] for i in range(2)]
    hv = hT.rearrange("(c p) t -> p c t", p=128)
    tmp_t = [P.sbuf([128, SEG + 1], F32, f"tmp{i}") for i in range(2)]
    tmp_b = [Buf(f"tmp{i}") for i in range(2)]
    pj = [P.sbuf([128, SEG], F32, f"pj{i}") for i in range(10)]
    pj_b = [Buf(f"pj{i}") for i in range(10)]
    lv_sb = P.sbuf([32, SEG], F32, "lv_sb")
    lv_b = Buf("lv")

    FM_NAMES = ["sg", "a", "kkn", "kmod", "b", "vm", "gate", "cl", "dexc", "Einc", "Eexc", "Eneg", "Ehat", "rev",
                "At", "Rt", "Bt", "Kt", "Bh", "Kh", "rk", "t1", "rs", "Y", "cent", "sqc", "bon", "vf", "mg"]
    FM = [{n: P.sbuf([128, SEG], F32, f"{n}{c}") for n in FM_NAMES} for c in range(2)]
    FMb = [{n: Buf(f"{n}{c}") for n in FM_NAMES} for c in range(2)]
    yo_t = [P.sbuf([128, SEG], BF16, f"yo{c}") for c in range(2)]
    yo_b = [Buf(f"yo{c}") for c in range(2)]
    TM_NAMES = ["Atok", "Bhtok", "Khtok", "Vtok"]
    TM = [{n: P.sbuf([64, NJ, 128], F32, f"{n}{c}") for n in TM_NAMES} for c in range(2)]
    TMb = [{n: Buf(f"{n}{c}") for n in TM_NAMES} for c in range(2)]
    MX_NAMES = ["P0", "Q0", "P1", "Q1", "TTa", "TTb", "Aak", "ArbT", "ArkT", "TAkT", "U0"]
    MX = [{n: P.sbuf([64, 2 * NJ, 64], F32, f"{n}{c}") for n in MX_NAMES} for c in range(2)]
    MXb = [{n: Buf(f"{n}{c}") for n in MX_NAMES} for c in range(2)]
    Apt = [P.sbuf([128, SEG], F32, f"Apt{c}") for c in range(2)]
    Apt_b = [Buf(f"Apt{c}") for c in range(2)]
    Sall = [[P.sbuf([128, (NJ + 1) * 64], F32, f"Sall{c}_{p}") for p in range(2)] for c in range(2)]
    Sall_b = [[[Buf(f"Sall{c}_{p}_{j}") for j in range(NJ + 1)] for p in range(2)] for c in range(2)]
    Uall = [P.sbuf([64, NJ, 2, 64], F32, f"Uall{c}") for c in range(2)]
    Uall_b = [[[Buf(f"Uall{c}_{j}_{h}") for h in range(2)] for j in range(NJ)] for c in range(2)]
    for c in range(2):
        k.memset(Sall[c][1][:, NJ * 64:(NJ + 1) * 64], 0.0, w=[Sall_b[c][1][NJ]])
    outs = []
    mI = ident_s[0:64, 0:64]

    def sslot(c, s, j):
        if j == 0:
            p = (s - 1) % 2
            return Sall[c][p][:, NJ * 64:(NJ + 1) * 64], Sall_b[c][p][NJ], p, NJ
        p = s % 2
        return Sall[c][p][:, j * 64:(j + 1) * 64], Sall_b[c][p][j], p, j

    for s in range(min(NSEG, RW_DBG["nseg"])):
        t0 = s * SEG
        hs_, hsb = hseg[s % 2], hseg_b[s % 2]
        for q in range(4):
            if s == 0:
                k.memset(hs_[:, 4 * q:4 * q + 4, 0:1], 0.0, w=[hsb[q]], eng="dve")
                k.dma(hs_[:, 4 * q:4 * q + 4, 1:SEG + 1], hv[:, 4 * q:4 * q + 4, 0:SEG], w=[hsb[q]])
            else:
                k.dma(hs_[:, 4 * q:4 * q + 4, :], hv[:, 4 * q:4 * q + 4, t0 - 1:t0 + SEG], w=[hsb[q]])
        for ch in range(10):
            ps, pb = k.bank()
            for kc in range(16):
                k.mm(ps[:, 0:SEG + 1], wr[:, kc, ch * 128:(ch + 1) * 128], hs_[:, kc, :], start=(kc == 0), stop=(kc == 15),
                     r=[wr_b[ch // 2][kc // 8], hsb[kc // 4]], w=[pb])
            tm, tmb = tmp_t[ch % 2], tmp_b[ch % 2]
            k.act(tm[:], ps[:, 0:SEG + 1], AF.Copy, r=[pb, cols_b], w=[tmb], scale=cols_s[:, MU + ch:MU + ch + 1])
            k.stt(pj[ch][:], ps[:, 1:SEG + 1], cols_s[:, OMM + ch:OMM + ch + 1], tm[:, 0:SEG], ALU.mult, ALU.add,
                  r=[pb, cols_b, tmb], w=[pj_b[ch]])
        if RW_DBG["stage"] < 2:
            continue
        k.act(pj[8][0:64, :], pj[8][0:64, :], AF.Tanh, r=[pj_b[8]], w=[pj_b[8]])
        k.act(pj[9][:], pj[9][:], AF.Sigmoid, r=[pj_b[9]], w=[pj_b[9]])
        if layer1:
            ps, pb = k.bank()
            for kc in range(4):
                k.mm(ps[0:32, 0:SEG], v1_s[:, kc, :], pj[4 + kc][:], start=(kc == 0), stop=(kc == 3), r=[v1_b, pj_b[4 + kc]], w=[pb])
            k.copy(lv_sb[:], ps[0:32, 0:SEG], r=[pb], w=[lv_b], eng="act")
        for c in range(2):
            f, fb = FM[c], FMb[c]
            cs = slice(c * 128, (c + 1) * 128)
            rp, rpb = pj[c], pj_b[c]
            kp, kpb = pj[2 + c], pj_b[2 + c]
            vp, vpb = pj[4 + c], pj_b[4 + c]
            col = lambda base: cols_s[:, base + c:base + c + 1]
            ps, pb = k.bank()
            k.mm(ps[:, 0:SEG], wa2_s[0:64, cs], pj[8][0:64, :], r=[wa2_b, pj_b[8]], w=[pb])
            k.act(f["sg"][:], ps[:, 0:SEG], AF.Sigmoid, r=[pb, cols_b], w=[fb["sg"]], bias=col(W0))
            ps, pb = k.bank()
            k.mm(ps[:, 0:SEG], wa2_s[64:128, cs], pj[8][64:128, :], r=[wa2_b, pj_b[8]], w=[pb])
            k.act(f["a"][:], ps[:, 0:SEG], AF.Sigmoid, r=[pb, cols_b], w=[fb["a"]], bias=col(A0))
            ps, pb = k.bank()
            k.mm(ps[:, 0:SEG], g2m_s[:, cs], pj[9][:], r=[g2m_b, pj_b[9]], w=[pb])
            k.copy(f["gate"][:], ps[:, 0:SEG], r=[pb], w=[fb["gate"]], eng="act")
            if layer1:
                k.dma(f["vf"][:], vf_in[c * 128:(c + 1) * 128, t0:t0 + SEG], w=[fb["vf"]])
                ps, pb = k.bank()
                k.mm(ps[:, 0:SEG], v2m_s[0:32, cs], lv_sb[0:32, :], r=[v2m_b, lv_b], w=[pb])
                k.act(f["mg"][:], ps[:, 0:SEG], AF.Sigmoid, r=[pb, cols2_b], w=[fb["mg"]], bias=cols2_s[:, V0 + c:V0 + c + 1])
                k.tt(f["vm"][:], f["vf"][:], vp[:], ALU.subtract, r=[fb["vf"], vpb], w=[fb["vm"]])
                k.tt(f["vm"][:], f["vm"][:], f["mg"][:], ALU.mult, r=[fb["vm"], fb["mg"]], w=[fb["vm"]])
                k.tt(f["vm"][:], f["vm"][:], vp[:], ALU.add, r=[fb["vm"], vpb], w=[fb["vm"]])
                vm, vmb = f["vm"], fb["vm"]
            else:
                outs.append(k.dma(vf_out[c * 128:(c + 1) * 128, t0:t0 + SEG], vp[:], r=[vpb]))
                vm, vmb = vp, vpb
            k.act(f["sqc"][:], kp[:], AF.Square, r=[kpb, cols_b], w=[fb["sqc"]], scale=col(KK))
            ps, pb = k.bank()
            k.mm(ps[:, 0:SEG], ob32_s[:], f["sqc"][:], r=[ob32_b, fb["sqc"]], w=[pb])
            k.ts(f["rs"][:], ps[:, 0:SEG], 1e-24, ALU.max, r=[pb], w=[fb["rs"]])
            k.act(f["rs"][:], f["rs"][:], AF.Sqrt, r=[fb["rs"]], w=[fb["rs"]])
            k.recip(f["rs"][:], f["rs"][:], r=[fb["rs"]], w=[fb["rs"]])
            k.stt(f["kkn"][:], kp[:], col(KK), f["rs"][:], ALU.mult, ALU.mult, r=[kpb, cols_b, fb["rs"]], w=[fb["kkn"]])
            k.ts(f["t1"][:], f["a"][:], -1.0, ALU.add, col(KA), ALU.mult, r=[fb["a"], cols_b], w=[fb["t1"]])
            k.stt(f["kmod"][:], f["t1"][:], 1.0, kp[:], ALU.add, ALU.mult, r=[fb["t1"], kpb], w=[fb["kmod"]])
            k.tt(f["b"][:], f["kkn"][:], f["a"][:], ALU.mult, r=[fb["kkn"], fb["a"]], w=[fb["b"]], eng="pool")
            k.stt(f["rk"][:], rp[:], col(RK), f["kmod"][:], ALU.mult, ALU.mult, r=[rpb, cols_b, fb["kmod"]], w=[fb["rk"]])
            P.op("dve", lambda e, o=f["cl"], m=cmask, d=f["sg"]: e.tensor_tensor_scan(o[:], m[:], d[:], 0.0, ALU.mult, ALU.add),
                 [cmask_b, fb["sg"]], [fb["cl"]])
            k.tt(f["dexc"][:], f["cl"][:], f["sg"][:], ALU.subtract, r=[fb["cl"], fb["sg"]], w=[fb["dexc"]], eng="pool")
            clv = f["cl"][:].rearrange("p (j t) -> p j t", t=CH)
            k.tt(f["rev"][:].rearrange("p (j t) -> p j t", t=CH), clv[:, :, CH - 1:CH].broadcast_to([128, NJ, CH]), clv, ALU.subtract,
                 r=[fb["cl"]], w=[fb["rev"]])
            k.act(f["Einc"][:], f["cl"][:], AF.Exp, r=[fb["cl"]], w=[fb["Einc"]], scale=-C0)
            k.act(f["Eexc"][:], f["dexc"][:], AF.Exp, r=[fb["dexc"]], w=[fb["Eexc"]], scale=-C0)
            k.act(f["Eneg"][:], f["cl"][:], AF.Exp, r=[fb["cl"]], w=[fb["Eneg"]], scale=C0)
            k.act(f["Ehat"][:], f["rev"][:], AF.Exp, r=[fb["rev"]], w=[fb["Ehat"]], scale=-C0)
            k.stt(f["At"][:], f["kkn"][:], -1.0, f["Eexc"][:], ALU.mult, ALU.mult, r=[fb["kkn"], fb["Eexc"]], w=[fb["At"]])
            k.tt(f["Rt"][:], rp[:], f["Einc"][:], ALU.mult, r=[rpb, fb["Einc"]], w=[fb["Rt"]], eng="pool")
            k.tt(f["Bt"][:], f["b"][:], f["Eneg"][:], ALU.mult, r=[fb["b"], fb["Eneg"]], w=[fb["Bt"]], eng="pool")
            k.tt(f["Kt"][:], f["kmod"][:], f["Eneg"][:], ALU.mult, r=[fb["kmod"], fb["Eneg"]], w=[fb["Kt"]], eng="pool")
            k.tt(f["Bh"][:], f["b"][:], f["Ehat"][:], ALU.mult, r=[fb["b"], fb["Ehat"]], w=[fb["Bh"]], eng="pool")
            k.tt(f["Kh"][:], f["kmod"][:], f["Ehat"][:], ALU.mult, r=[fb["kmod"], fb["Ehat"]], w=[fb["Kh"]])
            if RW_DBG["stage"] < 3:
                continue
            for nm, src, sb_ in (("Atok", f["At"], fb["At"]), ("Bhtok", f["Bh"], fb["Bh"]), ("Khtok", f["Kh"], fb["Kh"]), ("Vtok", vm, vmb)):
                ps, pb = k.bank()
                for j in range(NJ):
                    k.transpose(ps[0:64, j * 128:(j + 1) * 128], src[:, j * CH:(j + 1) * CH], ident_s[:], r=[sb_, ident_b], w=[pb])
                k.copy(TM[c][nm][:].rearrange("p j f -> p (j f)"), ps[0:64, 0:NJ * 128], r=[pb], w=[TMb[c][nm]], eng="act")
            if RW_DBG["stage"] < 4:
                continue
            m_, mb_ = MX[c], MXb[c]
            for nm, lt, ltb, rt, rtb, mk in (("Q0", "Bt", None, "At", None, 1), ("P0", "At", None, "Bt", None, 0),
                                             ("Aak", "At", None, "Kt", None, 0), ("ArbT", "Bt", None, "Rt", None, 2),
                                             ("ArkT", "Kt", None, "Rt", None, 2)):
                for h in range(2):
                    hp = slice(h * 64, (h + 1) * 64)
                    ps, pb = k.bank()
                    for j in range(NJ):
                        js = slice(j * CH, (j + 1) * CH)
                        k.mm(ps[0:64, j * 64:(j + 1) * 64], f[lt][hp, js], f[rt][hp, js], r=[fb[lt], fb[rt]], w=[pb])
                    k.tt(m_[nm][:, h * NJ:(h + 1) * NJ, :], ps[0:64, 0:NJ * 64].rearrange("p (j t) -> p j t", t=64),
                         masks_s[:, mk:mk + 1, :].broadcast_to([64, NJ, 64]), ALU.mult, r=[pb, masks_b], w=[mb_[nm]])
            if RW_DBG["stage"] < 5:
                continue
            NB = 2 * NJ
            k.tt(m_["TTa"][:], m_["Q0"][:], mI.unsqueeze(1).broadcast_to([64, NB, 64]), ALU.add, r=[mb_["Q0"], ident_b], w=[mb_["TTa"]])
            Pc, Qc, Pn, Qn, Tc, Tn = "P0", "Q0", "P1", "Q1", "TTa", "TTb"
            for lvl in range(1, 6):
                ps, pb = k.bank()
                for blk in range(NB):
                    k.mm(ps[0:64, blk * 64:(blk + 1) * 64], m_[Qc][:, blk, :], m_[Pc][:, blk, :], r=[mb_[Qc], mb_[Pc]], w=[pb])
                k.copy(m_[Pn][:].rearrange("p b t -> p (b t)"), ps[0:64, 0:NB * 64], r=[pb], w=[mb_[Pn]], eng="act")
                if lvl < 5:
                    ps, pb = k.bank()
                    for blk in range(NB):
                        k.mm(ps[0:64, blk * 64:(blk + 1) * 64], m_[Pc][:, blk, :], m_[Qc][:, blk, :], r=[mb_[Qc], mb_[Pc]], w=[pb])
                    k.copy(m_[Qn][:].rearrange("p b t -> p (b t)"), ps[0:64, 0:NB * 64], r=[pb], w=[mb_[Qn]])
                ps, pb = k.bank()
                for blk in range(NB):
                    k.mm(ps[0:64, blk * 64:(blk + 1) * 64], m_[Pn][:, blk, :], m_[Tc][:, blk, :], r=[mb_[Pn], mb_[Tc]], w=[pb])
                k.tt(m_[Tn][:].rearrange("p b t -> p (b t)"), ps[0:64, 0:NB * 64], m_[Tc][:].rearrange("p b t -> p (b t)"), ALU.add,
                     r=[pb, mb_[Tc]], w=[mb_[Tn]])
                Pc, Pn = Pn, Pc
                Qc, Qn = Qn, Qc
                Tc, Tn = Tn, Tc
            TT, TTb_ = m_[Tc], mb_[Tc]
            ps, pb = k.bank()
            for j in range(NJ):
                for h in range(2):
                    hp = slice(h * 64, (h + 1) * 64)
                    k.mm(ps[hp, j * 64:(j + 1) * 64], TM[c]["Atok"][:, j, h * 64:(h + 1) * 64], TT[:, h * NJ + j, :],
                         r=[TMb[c]["Atok"], TTb_], w=[pb])
            k.copy(Apt[c][:], ps[:, 0:SEG], r=[pb], w=[Apt_b[c]], eng="act")
            ps, pb = k.bank()
            for blk in range(NB):
                k.mm(ps[0:64, blk * 64:(blk + 1) * 64], m_["Aak"][:, blk, :], TT[:, blk, :], r=[mb_["Aak"], TTb_], w=[pb])
            k.copy(m_["TAkT"][:].rearrange("p b t -> p (b t)"), ps[0:64, 0:NB * 64], r=[pb], w=[mb_["TAkT"]])
            ps, pb = k.bank()
            for j in range(NJ):
                for h in range(2):
                    blk = h * NJ + j
                    k.mm(ps[0:64, blk * 64:(blk + 1) * 64], m_["TAkT"][:, blk, :], TM[c]["Vtok"][:, j, h * 64:(h + 1) * 64],
                         r=[mb_["TAkT"], TMb[c]["Vtok"]], w=[pb])
            k.copy(m_["U0"][:].rearrange("p b t -> p (b t)"), ps[0:64, 0:NB * 64], r=[pb], w=[mb_["U0"]], eng="act")

        if RW_DBG["stage"] < 6:
            continue
        for j in range(NJ):
            for c in range(2):
                f, fb = FM[c], FMb[c]
                sp_ap, sp_b, _, _ = sslot(c, s, j)
                for h in range(2):
                    hp = slice(h * 64, (h + 1) * 64)
                    ps, pb = k.bank()
                    k.mm(ps[0:64, 0:64], Apt[c][hp, j * 64:(j + 1) * 64], sp_ap[hp, :], start=True, stop=True,
                         r=[Apt_b[c], sp_b], w=[pb])
                    k.tt(Uall[c][:, j, h, :], ps[0:64, 0:64], MX[c]["U0"][:, h * NJ + j, :], ALU.add, r=[pb, MXb[c]["U0"]], w=[Uall_b[c][j][h]])
                ps, pb = k.bank()
                for h in range(2):
                    hp = slice(h * 64, (h + 1) * 64)
                    k.mm(ps[hp, 0:64], TM[c]["Khtok"][:, j, h * 64:(h + 1) * 64], TM[c]["Vtok"][:, j, h * 64:(h + 1) * 64], start=True, stop=False,
                         r=[TMb[c]["Khtok"], TMb[c]["Vtok"]], w=[pb])
                    k.mm(ps[hp, 0:64], TM[c]["Bhtok"][:, j, h * 64:(h + 1) * 64], Uall[c][:, j, h, :], start=False, stop=True,
                         r=[TMb[c]["Bhtok"], Uall_b[c][j][h]], w=[pb])
                pn = s % 2
                k.stt(Sall[c][pn][:, (j + 1) * 64:(j + 2) * 64], sp_ap, f["Einc"][:, j * CH + CH - 1:j * CH + CH], ps[:, 0:64],
                      ALU.mult, ALU.add, r=[sp_b, fb["Einc"], pb], w=[Sall_b[c][pn][j + 1]])
        if RW_DBG["stage"] < 7:
            continue
        for c in range(2):
            f, fb = FM[c], FMb[c]
            vm, vmb = (f["vm"], fb["vm"]) if layer1 else (pj[4 + c], pj_b[4 + c])
            ps1, pb1 = k.bank()
            for j in range(NJ):
                sp_ap, sp_b, _, _ = sslot(c, s, j)
                for h in range(2):
                    hp = slice(h * 64, (h + 1) * 64)
                    k.mm(ps1[hp, j * 64:(j + 1) * 64], sp_ap[hp, :], f["Rt"][hp, j * CH:(j + 1) * CH], r=[sp_b, fb["Rt"]], w=[pb1])
            ps2, pb2 = k.bank()
            for j in range(NJ):
                for h in range(2):
                    hp = slice(h * 64, (h + 1) * 64)
                    k.mm(ps2[hp, j * 64:(j + 1) * 64], Uall[c][:, j, h, :], MX[c]["ArbT"][:, h * NJ + j, :], start=True, stop=False,
                         r=[Uall_b[c][j][h], MXb[c]["ArbT"]], w=[pb2])
                    k.mm(ps2[hp, j * 64:(j + 1) * 64], TM[c]["Vtok"][:, j, h * 64:(h + 1) * 64], MX[c]["ArkT"][:, h * NJ + j, :], start=False, stop=True,
                         r=[TMb[c]["Vtok"], MXb[c]["ArkT"]], w=[pb2])
            k.copy(f["Y"][:], ps1[:, 0:SEG], r=[pb1], w=[fb["Y"]], eng="act")
            k.tt(f["Y"][:], f["Y"][:], ps2[:, 0:SEG], ALU.add, r=[fb["Y"], pb2], w=[fb["Y"]])
            ps, pb = k.bank()
            k.mm(ps[:, 0:SEG], ob64_s[:], f["Y"][:], r=[ob64_b, fb["Y"]], w=[pb])
            k.tt(f["cent"][:], f["Y"][:], ps[:, 0:SEG], ALU.subtract, r=[fb["Y"], pb], w=[fb["cent"]])
            k.act(f["sqc"][:], f["cent"][:], AF.Square, r=[fb["cent"]], w=[fb["sqc"]])
            ps, pb = k.bank()
            k.mm(ps[:, 0:SEG], ob64_s[:], f["sqc"][:], r=[ob64_b, fb["sqc"]], w=[pb])
            k.act(f["rs"][:], ps[:, 0:SEG], AF.Sqrt, r=[pb], w=[fb["rs"]], bias=LN_X_EPS)
            k.recip(f["rs"][:], f["rs"][:], r=[fb["rs"]], w=[fb["rs"]])
            k.tt(f["cent"][:], f["cent"][:], f["rs"][:], ALU.mult, r=[fb["cent"], fb["rs"]], w=[fb["cent"]])
            k.ts(f["cent"][:], f["cent"][:], cols2_s[:, LNG + c:LNG + c + 1], ALU.mult, cols2_s[:, LNB + c:LNB + c + 1], ALU.add,
                 r=[fb["cent"], cols2_b], w=[fb["cent"]])
            ps, pb = k.bank()
            k.mm(ps[:, 0:SEG], ob32_s[:], f["rk"][:], r=[ob32_b, fb["rk"]], w=[pb])
            k.tt(f["bon"][:], ps[:, 0:SEG], vm[:], ALU.mult, r=[pb, vmb], w=[fb["bon"]])
            k.tt(f["cent"][:], f["cent"][:], f["bon"][:], ALU.add, r=[fb["cent"], fb["bon"]], w=[fb["cent"]])
            k.tt(yo_t[c][:], f["cent"][:], f["gate"][:], ALU.mult, r=[fb["cent"], fb["gate"]], w=[yo_b[c]])
            outs.append(k.dma(yT[c * 128:(c + 1) * 128, t0:t0 + SEG], yo_t[c][:], r=[yo_b[c]]))

    P.wait_all("sp", outs)
    P.emit()
    P.close()
    return nc


def pack_rwkv_inputs(inp, l, hh, consts):
    w = inp["w_in"][l]
    own = lambda base, c: slice(base + 256 * hh + 128 * c, base + 256 * hh + 128 * (c + 1))
    oth = lambda base, c: slice(base + 256 * (1 - hh) + 128 * c, base + 256 * (1 - hh) + 128 * (c + 1))
    colsl = [own(OFF["r_r"], 0), own(OFF["r_r"], 1), own(OFF["r_k"], 0), own(OFF["r_k"], 1),
             own(OFF["r_v"], 0), own(OFF["r_v"], 1), oth(OFF["r_v"], 0), oth(OFF["r_v"], 1),
             slice(OFF["r_wd"], OFF["r_wd"] + 128), slice(OFF["r_gd"], OFF["r_gd"] + 128)]
    w_r = np.ascontiguousarray(np.concatenate([w[:, s_] for s_ in colsl], axis=1))
    mu = inp["rwkv_mu"][l]
    rc = np.zeros((128, 32), np.float32)
    for i, s_ in enumerate(colsl):
        rc[:, i] = mu[s_.start - 3840:s_.stop - 3840]
    hs = lambda v, c: v[256 * hh + 128 * c:256 * hh + 128 * (c + 1)]
    rkf = inp["r_k"][l].reshape(-1)
    for c in range(2):
        rc[:, 20 + c] = hs(inp["decay_w0"][l], c)
        rc[:, 22 + c] = hs(inp["iclr_a0"][l], c)
        rc[:, 24 + c] = hs(inp["k_k"][l], c)
        rc[:, 26 + c] = hs(inp["k_a"][l], c)
        rc[:, 28 + c] = hs(rkf, c)
    rc2 = np.zeros((128, 8), np.float32)
    for c in range(2):
        rc2[:, c] = hs(inp["ln_x_g"][l], c)
        rc2[:, 2 + c] = hs(inp["ln_x_b"][l], c)
        if l > 0:
            rc2[:, 4 + c] = hs(inp["vres_v0"][l - 1], c)
    d = {"w_r": w_r, "rcols": rc, "rcols2": rc2,
         "wa2": np.ascontiguousarray(np.concatenate([inp["decay_w2"][l][:, 256 * hh:256 * hh + 256],
                                                     inp["iclr_a2"][l][:, 256 * hh:256 * hh + 256]], axis=0)),
         "g2m": np.ascontiguousarray(inp["gate_g2"][l][:, 256 * hh:256 * hh + 256])}
    if l > 0:
        v1 = inp["vres_v1"][l - 1]
        rows = np.concatenate([np.arange(256 * hh, 256 * hh + 256), np.arange(256 * (1 - hh), 256 * (1 - hh) + 256)])
        d["v1p"] = np.ascontiguousarray(v1[rows].reshape(4, 128, 32).transpose(1, 0, 2))
        d["v2m"] = np.ascontiguousarray(inp["vres_v2"][l - 1][:, 256 * hh:256 * hh + 256])
    d.update(consts)
    return d


def _run(nc, in_maps):
    res = run_bass_kernel_spmd(nc, in_maps, core_ids=list(range(NCORES)))
    return res.results


def kernel_unfused(**inp):
    inp = {k_: np.asarray(v) for k_, v in inp.items()}
    x = inp["x"]
    B = x.shape[0]
    cores = [(b, hh) for b in range(B) for hh in range(2)]
    ac, rc = attn_consts(), rwkv_consts()
    xT = [np.ascontiguousarray(x[b, 1024 * th:1024 * (th + 1)].T) for (b, th) in cores]
    nc_norm = build_dense(False, only_norm=True)
    r = _run(nc_norm, [{"xT": xT[i], "g_mix": gcol(inp["norm_mix"][0])} for i in range(NCORES)])
    h_half = [r[i]["h_out"] for i in range(NCORES)]
    nc_attn = build_attn()
    vf = None
    for l in range(2):
        hfull = [np.ascontiguousarray(np.concatenate([h_half[2 * b], h_half[2 * b + 1]], axis=1)) for b in range(B)]
        ims = []
        for (b, hh) in cores:
            d = pack_attn_inputs(inp, l, hh, ac)
            d["hT"] = hfull[b]
            ims.append(d)
        ra = _run(nc_attn, ims)
        nc_rw = build_rwkv(l > 0)
        ims = []
        for i, (b, hh) in enumerate(cores):
            d = pack_rwkv_inputs(inp, l, hh, rc)
            d["hT"] = hfull[b]
            if l > 0:
                d["vf_in"] = vf[i]
            ims.append(d)
        rr = _run(nc_rw, ims)
        if l == 0:
            vf = [rr[i]["vf_out"] for i in range(NCORES)]
        ims = []
        for i, (b, th) in enumerate(cores):
            parts = []
            for m in range(3):
                for hh in range(2):
                    parts.append(ra[2 * b + hh]["yT_a"][256 * m:256 * (m + 1)])
            for hh in range(2):
                parts.append(rr[2 * b + hh]["yT_r"])
            ycat = np.concatenate(parts, axis=0)
            d = {"xT": xT[i], "yT": np.ascontiguousarray(ycat[:, 1024 * th:1024 * (th + 1)]),
                 "w_out": inp["w_out"][l], "w_up": inp["w_up"][l], "w_down": inp["w_down"][l], "g_ffn": gcol(inp["norm_ffn"][l])}
            if l == 0:
                d["g_mix"] = gcol(inp["norm_mix"][1])
            ims.append(d)
        nc_d = build_dense(l == 0)
        rd = _run(nc_d, ims)
        xT = [rd[i]["x_out"] for i in range(NCORES)]
        if l == 0:
            h_half = [rd[i]["h_out"] for i in range(NCORES)]
    out = np.empty_like(x)
    for i, (b, th) in enumerate(cores):
        out[b, 1024 * th:1024 * (th + 1)] = xT[i].T
    return out


def kernel(**inputs):
    return kernel_unfused(**inputs)
```
